# Optimizing a Trainium2 kernel written in Bass

```python
import math
import jax, jax.numpy as jnp
from jax import lax
import numpy as np

D_MODEL = 1024
BATCH = 1
SEQ = 16384
DEPTH = 4

N_MIXERS = 2
N_SSD_LAYERS = (DEPTH + 1) // 2
N_SC_LAYERS = DEPTH // 2
RMS_EPS = 1e-5
D_FF = 2816
SSD_EXPAND = 2
SSD_D_INNER = SSD_EXPAND * D_MODEL
SSD_HEAD_DIM = 64
SSD_N_HEADS = SSD_D_INNER // SSD_HEAD_DIM
SSD_N_GROUPS = 4
SSD_HEADS_PER_GROUP = SSD_N_HEADS // SSD_N_GROUPS
SSD_D_STATE = 128
SSD_CONV_W = 4
SSD_CHUNK = 128
SSD_CONV_DIM = SSD_D_INNER + 2 * SSD_N_GROUPS * SSD_D_STATE
SSD_IN_DIM = SSD_D_INNER + SSD_CONV_DIM + SSD_N_HEADS
SSD_DT_MIN = 1e-3
SSD_DT_MAX = 1e-1
SC_CONV_W = 3

kernel_name = "hybrid_ssd_shortconv_macaron"


def rmsnorm(x, w):
    xf = x.astype(jnp.float32)
    inv = lax.rsqrt(jnp.mean(xf * xf, axis=-1, keepdims=True) + RMS_EPS)
    return (xf * inv).astype(x.dtype) * w


def swiglu(h, w_gate, w_up, w_down):
    return (jax.nn.silu(h @ w_gate) * (h @ w_up)) @ w_down


def causal_dwconv(u, w):
    k_w = w.shape[0]
    s = u.shape[1]
    upad = jnp.pad(u, ((0, 0), (k_w - 1, 0), (0, 0)))
    out = upad[:, 0:s] * w[0]
    for k in range(1, k_w):
        out = out + upad[:, k:k + s] * w[k]
    return out


def causal_decay(a_cs):
    t = a_cs.shape[-1]
    seg = a_cs[..., :, None] - a_cs[..., None, :]
    mask = jnp.tril(jnp.ones((t, t), dtype=bool))
    return jnp.exp(jnp.where(mask, seg, -jnp.inf))


def ssd_chunked(x, dt, a, bm, cm):
    b, s = x.shape[0], x.shape[1]
    nc = s // SSD_CHUNK
    G, E, P, L = SSD_N_GROUPS, SSD_HEADS_PER_GROUP, SSD_HEAD_DIM, SSD_CHUNK
    x_dt = x.astype(jnp.float32) * dt[..., None]
    xc = x_dt.reshape(b, nc, L, G, E, P)
    ac = (dt * a).reshape(b, nc, L, G, E).transpose(0, 3, 4, 1, 2)
    bc = bm.astype(jnp.float32).reshape(b, nc, L, G, SSD_D_STATE)
    cc = cm.astype(jnp.float32).reshape(b, nc, L, G, SSD_D_STATE)
    a_cs = jnp.cumsum(ac, axis=-1)
    lmat = causal_decay(a_cs)
    cb = jnp.einsum("bclgn,bcsgn->bgcls", cc, bc)
    y_diag = jnp.einsum("bgcls,bgecls,bcsgep->bclgep", cb, lmat, xc)
    decay_states = jnp.exp(a_cs[..., -1:] - a_cs)
    states = jnp.einsum("bclgn,bgecl,bclgep->bcgepn", bc, decay_states, xc)
    states = jnp.concatenate([jnp.zeros_like(states[:, :1]), states], axis=1)
    chunk_tot = jnp.pad(a_cs[..., -1], ((0, 0), (0, 0), (0, 0), (1, 0)))
    decay_chunk = causal_decay(jnp.cumsum(chunk_tot, axis=-1))
    new_states = jnp.einsum("bgezc,bcgepn->bzgepn", decay_chunk, states)
    prev_states = new_states[:, :-1]
    y_off = jnp.einsum("bclgn,bcgepn,bgecl->bclgep", cc, prev_states, jnp.exp(a_cs))
    return (y_diag + y_off).reshape(b, s, SSD_N_HEADS, P)


def ssd_mixer(h, w_in, conv_w, conv_b, dt_bias, a_log, d_skip, norm_w, w_out):
    b, s, _ = h.shape
    zxbcdt = h @ w_in
    z = zxbcdt[..., :SSD_D_INNER]
    xbc = zxbcdt[..., SSD_D_INNER:SSD_D_INNER + SSD_CONV_DIM]
    dt_raw = zxbcdt[..., SSD_D_INNER + SSD_CONV_DIM:]
    xbc = jax.nn.silu(causal_dwconv(xbc, conv_w) + conv_b)
    gn = SSD_N_GROUPS * SSD_D_STATE
    xs = xbc[..., :SSD_D_INNER].reshape(b, s, SSD_N_HEADS, SSD_HEAD_DIM)
    bm = xbc[..., SSD_D_INNER:SSD_D_INNER + gn].reshape(b, s, SSD_N_GROUPS, SSD_D_STATE)
    cm = xbc[..., SSD_D_INNER + gn:].reshape(b, s, SSD_N_GROUPS, SSD_D_STATE)
    dt = jax.nn.softplus(dt_raw.astype(jnp.float32) + dt_bias.astype(jnp.float32))
    a = -jnp.exp(a_log.astype(jnp.float32))
    y = ssd_chunked(xs, dt, a, bm, cm)
    y = y + xs.astype(jnp.float32) * d_skip.astype(jnp.float32)[:, None]
    y = y.reshape(b, s, SSD_D_INNER)
    g = y * jax.nn.silu(z.astype(jnp.float32))
    gg = g.reshape(b, s, SSD_N_GROUPS, SSD_D_INNER // SSD_N_GROUPS)
    gg = gg * lax.rsqrt(jnp.mean(gg * gg, axis=-1, keepdims=True) + RMS_EPS)
    g = gg.reshape(b, s, SSD_D_INNER).astype(h.dtype) * norm_w
    return g @ w_out


def shortconv_mixer(h, w_in, conv_w, w_out):
    bcu = h @ w_in
    bg = bcu[..., :D_MODEL]
    cg = bcu[..., D_MODEL:2 * D_MODEL]
    u = bcu[..., 2 * D_MODEL:]
    v = causal_dwconv(cg * u, conv_w)
    return (bg * v) @ w_out


def setup_inputs(seed: int = 0) -> dict:
    key = jax.random.key(seed)
    ks = jax.random.split(key, 20)
    f32 = jnp.float32
    nrm = lambda k, shape, fan_in: jax.random.normal(k, shape, f32) * (fan_in ** -0.5)
    x = jax.random.normal(ks[0], (BATCH, SEQ, D_MODEL), f32)
    norm_w = 1.0 + 0.01 * jax.random.normal(ks[1], (DEPTH, 3, D_MODEL), f32)
    ffn_w_gate = nrm(ks[2], (DEPTH, 2, D_MODEL, D_FF), D_MODEL)
    ffn_w_up = nrm(ks[3], (DEPTH, 2, D_MODEL, D_FF), D_MODEL)
    ffn_w_down = nrm(ks[4], (DEPTH, 2, D_FF, D_MODEL), D_FF)
    ssd_w_in = nrm(ks[5], (N_SSD_LAYERS, D_MODEL, SSD_IN_DIM), D_MODEL)
    ssd_conv_w = nrm(ks[6], (N_SSD_LAYERS, SSD_CONV_W, SSD_CONV_DIM), SSD_CONV_W)
    ssd_conv_b = 0.01 * jax.random.normal(ks[7], (N_SSD_LAYERS, SSD_CONV_DIM), f32)
    dt0 = jnp.exp(jax.random.uniform(ks[8], (N_SSD_LAYERS, SSD_N_HEADS), f32)
                  * (math.log(SSD_DT_MAX) - math.log(SSD_DT_MIN)) + math.log(SSD_DT_MIN))
    ssd_dt_bias = dt0 + jnp.log(-jnp.expm1(-dt0))
    ssd_a_log = jnp.log(jax.random.uniform(ks[9], (N_SSD_LAYERS, SSD_N_HEADS), f32, 1.0, 16.0))
    ssd_d = 1.0 + 0.1 * jax.random.normal(ks[10], (N_SSD_LAYERS, SSD_N_HEADS), f32)
    ssd_norm_w = 1.0 + 0.01 * jax.random.normal(ks[11], (N_SSD_LAYERS, SSD_D_INNER), f32)
    ssd_w_out = nrm(ks[12], (N_SSD_LAYERS, SSD_D_INNER, D_MODEL), SSD_D_INNER)
    sc_w_in = nrm(ks[13], (N_SC_LAYERS, D_MODEL, 3 * D_MODEL), D_MODEL)
    sc_conv_w = nrm(ks[14], (N_SC_LAYERS, SC_CONV_W, D_MODEL), SC_CONV_W)
    sc_w_out = nrm(ks[15], (N_SC_LAYERS, D_MODEL, D_MODEL), D_MODEL)
    final_norm_w = 1.0 + 0.01 * jax.random.normal(ks[16], (D_MODEL,), f32)
    return {"x": x, "norm_w": norm_w, "ffn_w_gate": ffn_w_gate, "ffn_w_up": ffn_w_up,
            "ffn_w_down": ffn_w_down, "ssd_w_in": ssd_w_in, "ssd_conv_w": ssd_conv_w,
            "ssd_conv_b": ssd_conv_b, "ssd_dt_bias": ssd_dt_bias, "ssd_a_log": ssd_a_log,
            "ssd_d": ssd_d, "ssd_norm_w": ssd_norm_w, "ssd_w_out": ssd_w_out,
            "sc_w_in": sc_w_in, "sc_conv_w": sc_conv_w, "sc_w_out": sc_w_out,
            "final_norm_w": final_norm_w}


def reference(x, norm_w, ffn_w_gate, ffn_w_up, ffn_w_down, ssd_w_in, ssd_conv_w, ssd_conv_b,
              ssd_dt_bias, ssd_a_log, ssd_d, ssd_norm_w, ssd_w_out, sc_w_in, sc_conv_w,
              sc_w_out, final_norm_w):
    for i in range(DEPTH):
        x = x + 0.5 * swiglu(rmsnorm(x, norm_w[i, 0]), ffn_w_gate[i, 0], ffn_w_up[i, 0], ffn_w_down[i, 0])
        h = rmsnorm(x, norm_w[i, 1])
        j = i // N_MIXERS
        if i % N_MIXERS == 0:
            mix = ssd_mixer(h, ssd_w_in[j], ssd_conv_w[j], ssd_conv_b[j], ssd_dt_bias[j],
                            ssd_a_log[j], ssd_d[j], ssd_norm_w[j], ssd_w_out[j])
        else:
            mix = shortconv_mixer(h, sc_w_in[j], sc_conv_w[j], sc_w_out[j])
        x = x + mix
        x = x + 0.5 * swiglu(rmsnorm(x, norm_w[i, 2]), ffn_w_gate[i, 1], ffn_w_up[i, 1], ffn_w_down[i, 1])
    return rmsnorm(x, final_norm_w)
```

```python
import numpy as np
from contextlib import ExitStack
import concourse.bass as bass
import concourse.mybir as mybir
from concourse.bass_utils import run_bass_kernel_spmd

F32 = mybir.dt.float32
BF16 = mybir.dt.bfloat16
AF = mybir.ActivationFunctionType
ALU = mybir.AluOpType

NCORES = 8
D = 1024
KD = 8
SEQ = 16384
T = SEQ // NCORES
HALO = 4
TT = T + HALO
DFF = 2816
NF = DFF // 128
DIN = 2048
NH = 32
HP = 64
NG = 4
NS = 128
CONV = 3072
SSD_IN = 5152
EPS = 1e-5
FFN_GROUPS = [(0, 4), (4, 4), (8, 4), (12, 4), (16, 4), (20, 2)]

TILES_H = [(0, HALO)] + [(HALO + 512 * i, 512) for i in range(4)]
TILES = [(HALO + 512 * i, 512) for i in range(4)]


class Buf:
    __slots__ = ("name", "w", "r")

    def __init__(self, name):
        self.name = name
        self.w = None
        self.r = {}


class Prog:
    ENG = ["pe", "act", "dve", "pool", "sp"]

    def __init__(self, nc):
        self.nc = nc
        self.ops = {e: [] for e in self.ENG}
        self.sig = set()
        self.dma_sems = {}
        self.bufs = {}
        self.stack = ExitStack()
        self.same_engine_sync = True

    def sbuf(self, name, shape, dtype):
        return self.stack.enter_context(self.nc.sbuf_tensor("sb_" + name, shape, dtype))

    def psum(self, name, shape, dtype):
        return self.stack.enter_context(self.nc.psum_tensor("ps_" + name, shape, dtype))

    def B(self, *key):
        b = self.bufs.get(key)
        if b is None:
            b = Buf("_".join(str(k) for k in key))
            self.bufs[key] = b
        return b

    def _deps(self, eng, reads, writes):
        deps = set()
        for b in reads:
            if b.w is not None:
                deps.add(b.w)
        for b in writes:
            if b.w is not None:
                deps.add(b.w)
            deps.update(b.r.values())
        if eng == "pe" or not self.same_engine_sync:
            deps = {d for d in deps if not (d[0] == "e" and d[1] == eng)}
        for d in deps:
            if d[0] == "e":
                self.sig.add((d[1], d[2]))
        return deps

    def op(self, eng, fn, reads=(), writes=()):
        deps = self._deps(eng, reads, writes)
        me = ("e", eng, len(self.ops[eng]))
        self.ops[eng].append((fn, deps, None))
        for b in reads:
            b.r[eng] = me
        for b in writes:
            b.w = me
            b.r = {}

    def dma(self, queue, fn, reads=(), writes=(), sem=None):
        deps = self._deps(queue + "_q", reads, writes)
        name = sem if sem is not None else (writes[0].name if writes else reads[0].name)
        cnt = self.dma_sems.get(name, 0) + 16
        self.dma_sems[name] = cnt
        me = ("d", name, cnt)
        self.ops[queue].append((fn, deps, name))
        for b in reads:
            b.r["d:" + name] = me
        for b in writes:
            b.w = me
            b.r = {}
        return me

    def phase_begin(self):
        self.pstack = ExitStack()

    def psbuf(self, name, shape, dtype):
        self.pcount = getattr(self, "pcount", 0) + 1
        return self.pstack.enter_context(self.nc.sbuf_tensor("sp%d_%s" % (self.pcount, name), shape, dtype))

    def phase_end(self):
        self.barrier()
        self.pstack.close()

    def barrier(self):
        deps = set()
        for e in self.ENG:
            for i in range(len(self.ops[e]) - 1, -1, -1):
                fn, _, dname = self.ops[e][i]
                if fn is not None and dname is None:
                    deps.add(("e", e, i))
                    self.sig.add((e, i))
                    break
        for name, cnt in self.dma_sems.items():
            deps.add(("d", name, cnt))
        for e in self.ENG:
            self.ops[e].append((None, set(deps), None))

    def final_wait(self, eng, bufs):
        deps = set()
        for b in bufs:
            if b.w is not None:
                deps.add(b.w)
            deps.update(b.r.values())
        for d in deps:
            if d[0] == "e":
                self.sig.add((d[1], d[2]))
        self.ops[eng].append((None, deps, None))

    def emit(self):
        nc = self.nc
        sigval = {}
        for e in self.ENG:
            c = 0
            for i in range(len(self.ops[e])):
                if (e, i) in self.sig:
                    c += 1
                    sigval[(e, i)] = c
        st = self.stack
        esem = {e: st.enter_context(nc.semaphore("s_" + e)) for e in self.ENG}
        dsem = {n: st.enter_context(nc.semaphore("d_" + n)) for n in self.dma_sems}
        block = st.enter_context(nc.Block())
        ops = self.ops
        sig = self.sig

        def run(e, eng):
            waited = {}
            for i, (fn, deps, dname) in enumerate(ops[e]):
                need = {}
                for d in deps:
                    if d[0] == "e":
                        key = ("e", d[1])
                        v = sigval[(d[1], d[2])]
                    else:
                        key = ("d", d[1])
                        v = d[2]
                    if need.get(key, 0) < v:
                        need[key] = v
                for key, v in need.items():
                    if waited.get(key, 0) < v:
                        eng.wait_ge(esem[key[1]] if key[0] == "e" else dsem[key[1]], v)
                        waited[key] = v
                if fn is None:
                    continue
                ins = fn(eng)
                if dname is not None:
                    ins.then_inc(dsem[dname], 16)
                elif (e, i) in sig:
                    ins.then_inc(esem[e], 1)

        @block.tensor
        def _(eng):
            run("pe", eng)

        @block.scalar
        def _(eng):
            run("act", eng)

        @block.vector
        def _(eng):
            run("dve", eng)

        @block.gpsimd
        def _(eng):
            run("pool", eng)

        @block.sync
        def _(eng):
            run("sp", eng)

    def close(self):
        self.stack.close()


class State:
    pass


def setup_state(P, smalls_d, consts_d, nsmall):
    S = State()
    S.x = P.sbuf("x", [128, KD, TT], F32)
    S.smalls = P.sbuf("smalls", [128, nsmall], F32)
    S.cbf = P.sbuf("cbf", [128, 3, 128], BF16)
    S.cf32 = P.sbuf("cf32", [128, 4, 128], F32)
    S.banks = [P.psum("bank%d" % i, [128, 512], F32) for i in range(8)]
    S.xsq = [P.sbuf("xsq%d" % i, [128, KD, 512], BF16) for i in range(1)]
    S.rs = [P.sbuf("rs%d" % i, [128, 512], F32) for i in range(2)]
    S.ctr = {}
    P.dma("sp", lambda e: e.dma_start(out=S.smalls[:], in_=smalls_d), writes=[P.B("smalls")])
    P.dma("sp", lambda e: e.dma_start(out=S.cf32[:], in_=consts_d), writes=[P.B("cf32")])
    P.op("dve", lambda e: e.tensor_copy(out=S.cbf[:, 0, :], in_=S.cf32[:, 2, :]), reads=[P.B("cf32")], writes=[P.B("cbf")])
    P.op("dve", lambda e: e.tensor_copy(out=S.cbf[:, 1, :], in_=S.cf32[:, 0, :]), reads=[P.B("cf32")], writes=[P.B("cbf")])
    return S


def nxt(S, name, n):
    v = S.ctr.get(name, 0)
    S.ctr[name] = v + 1
    return v % n


def sm(S, col, n=1, parts=128):
    return S.smalls[0:parts, col:col + n]


def MM(out, lhsT, rhs, start=True, stop=True):
    return lambda e: e.matmul(out, lhsT=lhsT, rhs=rhs, start=start, stop=stop)


def TR(out, in_, ident):
    return lambda e: e.transpose(out, in_, ident)


def ACT(out, in_, func, **kw):
    return lambda e: e.activation(out=out, in_=in_, func=func, **kw)


def TTo(out, in0, in1, op):
    return lambda e: e.tensor_tensor(out=out, in0=in0, in1=in1, op=op)


def STT(out, in0, scalar, in1, op0, op1):
    return lambda e: e.scalar_tensor_tensor(out=out, in0=in0, scalar=scalar, in1=in1, op0=op0, op1=op1)


def TS(out, in0, s1, s2, op0, op1=None):
    if op1 is None:
        return lambda e: e.tensor_scalar(out=out, in0=in0, scalar1=s1, scalar2=None, op0=op0)
    return lambda e: e.tensor_scalar(out=out, in0=in0, scalar1=s1, scalar2=s2, op0=op0, op1=op1)


def CP(out, in_):
    return lambda e: e.tensor_copy(out=out, in_=in_)


def RCP(out, in_):
    return lambda e: e.reciprocal(out=out, in_=in_)


def DMA(out, in_):
    return lambda e: e.dma_start(out=out, in_=in_, max_dma_last_dim=4096)


def emit_load_x(P, S, x_d):
    for k in range(KD):
        P.dma("sp", DMA(S.x[:, k, :], x_d[:, k, :]),
              writes=[P.B("x", k, ti) for ti in range(len(TILES_H))], sem="xload%d" % k)


def tidx(c0):
    return 0 if c0 == 0 else 1 + (c0 - HALO) // 512


def emit_norm(P, S, nwcol, tiles, dst=None, banks=(4, 5, 6, 7), eps_col=0):
    if dst is None:
        dst = lambda k, c0, w: (S.h[:, k, c0:c0 + w], [P.B("h", k, tidx(c0))])
    for (c0, w) in tiles:
        ti = tidx(c0)
        q = nxt(S, "xsq", 1)
        xsq = S.xsq[q]
        for k in range(KD):
            P.op("act", ACT(xsq[:, k, 0:w], S.x[:, k, c0:c0 + w], AF.Square),
                 reads=[P.B("x", k, ti)], writes=[P.B("xsq", q, k)])
        bk = banks[nxt(S, "nbank", len(banks)) % len(banks)]
        bank = S.banks[bk]
        for k in range(KD):
            P.op("pe", MM(bank[:, 0:w], S.cbf[:, 0, :], xsq[:, k, 0:w], k == 0, k == KD - 1),
                 reads=[P.B("xsq", q, k), P.B("cbf")], writes=[P.B("bank", bk)])
        rq = nxt(S, "rs", 2)
        rs = S.rs[rq]
        P.op("act", ACT(rs[:, 0:w], bank[:, 0:w], AF.Sqrt, bias=sm(S, eps_col), scale=1.0 / D),
             reads=[P.B("bank", bk), P.B("smalls")], writes=[P.B("rs", rq)])
        P.op("dve", RCP(rs[:, 0:w], rs[:, 0:w]), reads=[P.B("rs", rq)], writes=[P.B("rs", rq)])
        for k in range(KD):
            d_ap, wb = dst(k, c0, w)
            P.op("dve", STT(d_ap, S.x[:, k, c0:c0 + w], sm(S, nwcol + k), rs[:, 0:w], ALU.mult, ALU.mult),
                 reads=[P.B("x", k, ti), P.B("rs", rq), P.B("smalls")], writes=wb)


def setup_ffn(P, S):
    S.h = P.psbuf("h", [128, KD, TT], BF16)
    S.wg = [P.psbuf("wg%d" % i, [128, KD, 128], BF16) for i in range(3)]
    S.wu = [P.psbuf("wu%d" % i, [128, KD, 128], BF16) for i in range(3)]
    S.wd = [P.psbuf("wd%d" % i, [128, 4, D], BF16) for i in range(2)]
    S.a = [P.psbuf("a%d" % i, [128, 4, TT], BF16) for i in range(2)]
    S.st = [P.psbuf("st%d" % i, [128, 512], F32) for i in range(3)]


def emit_ffn(P, S, wg_d, wu_d, wd_d, nwcol, tiles):
    P.phase_begin()
    setup_ffn(P, S)
    emit_norm(P, S, nwcol, tiles)
    for (j0, G) in FFN_GROUPS:
        ab = nxt(S, "a", 2)
        ds = nxt(S, "wd", 2)
        P.dma("pool", DMA(S.wd[ds][:, 0:G, :], wd_d[j0:j0 + G].rearrange("j p d -> p j d")),
              writes=[P.B("wd", ds)])
        for jj in range(G):
            j = j0 + jj
            ws = nxt(S, "wgu", 3)
            P.dma("pool", DMA(S.wg[ws][:], wg_d[j]), writes=[P.B("wg", ws)])
            P.dma("pool", DMA(S.wu[ws][:], wu_d[j]), writes=[P.B("wu", ws)])
            for (c0, w) in tiles:
                ti = tidx(c0)
                bs = 2 * nxt(S, "gubank", 2)
                pg, pu = S.banks[bs], S.banks[bs + 1]
                for k in range(KD):
                    P.op("pe", MM(pg[:, 0:w], S.wg[ws][:, k, :], S.h[:, k, c0:c0 + w], k == 0, k == KD - 1),
                         reads=[P.B("wg", ws), P.B("h", k, ti)], writes=[P.B("bank", bs)])
                for k in range(KD):
                    P.op("pe", MM(pu[:, 0:w], S.wu[ws][:, k, :], S.h[:, k, c0:c0 + w], k == 0, k == KD - 1),
                         reads=[P.B("wu", ws), P.B("h", k, ti)], writes=[P.B("bank", bs + 1)])
                sq = nxt(S, "st", 3)
                stt = S.st[sq]
                P.op("act", ACT(stt[:, 0:w], pg[:, 0:w], AF.Silu),
                     reads=[P.B("bank", bs)], writes=[P.B("st", sq)])
                P.op("dve", TTo(S.a[ab][:, jj, c0:c0 + w], pu[:, 0:w], stt[:, 0:w], ALU.mult),
                     reads=[P.B("bank", bs + 1), P.B("st", sq)], writes=[P.B("a", ab, jj, ti)])
        for (c0, w) in tiles:
            ti = tidx(c0)
            for dc in range(KD):
                bk = 4 + nxt(S, "dbank", 4)
                bank = S.banks[bk]
                for jj in range(G):
                    P.op("pe", MM(bank[:, 0:w], S.wd[ds][:, jj, dc * 128:(dc + 1) * 128], S.a[ab][:, jj, c0:c0 + w],
                                  jj == 0, jj == G - 1),
                         reads=[P.B("wd", ds), P.B("a", ab, jj, ti)], writes=[P.B("bank", bk)])
                P.op("dve", STT(S.x[:, dc, c0:c0 + w], bank[:, 0:w], 0.5, S.x[:, dc, c0:c0 + w], ALU.mult, ALU.add),
                     reads=[P.B("bank", bk), P.B("x", dc, ti)], writes=[P.B("x", dc, ti)])
    P.phase_end()


def emit_store_x(P, S, xo_d, with_halo):
    outs = []
    for k in range(KD):
        b = P.B("xstore", k)
        src = S.x[:, k, :] if with_halo else S.x[:, k, HALO:TT]
        P.dma("sp", DMA(xo_d[:, k, :], src),
              reads=[P.B("x", k, ti) for ti in range(len(TILES_H))], writes=[b], sem="xstore%d" % k)
        outs.append(b)
    return outs


def emit_sc(P, S, win_d, wout_d, nwcol, cwcol):
    P.phase_begin()
    A = P.psbuf
    S.h = A("h", [128, KD, TT], BF16)
    win = [A("win%d" % i, [128, KD, 128], BF16) for i in range(4)]
    cu = [A("cu%d" % i, [128, TT], F32) for i in range(2)]
    v = [A("v%d" % i, [128, T], F32) for i in range(1)]
    tmp = [A("tmp%d" % i, [128, 512], F32) for i in range(2)]
    bv = A("bv", [128, KD, T], BF16)
    wo = [A("wo%d" % i, [128, KD, 128], BF16) for i in range(2)]
    emit_norm(P, S, nwcol, TILES_H)

    def load_w(j):
        s = nxt(S, "scwin", 4)
        P.dma("pool", DMA(win[s][:], win_d[j]), writes=[P.B("scwin", s)])
        return s

    for m in range(KD):
        s_c = load_w(8 + m)
        s_u = load_w(16 + m)
        cq = nxt(S, "cu", 2)
        for (c0, w) in TILES_H:
            ti = tidx(c0)
            bs = 2 * nxt(S, "gubank", 2)
            pc, pu = S.banks[bs], S.banks[bs + 1]
            for k in range(KD):
                P.op("pe", MM(pc[:, 0:w], win[s_c][:, k, :], S.h[:, k, c0:c0 + w], k == 0, k == KD - 1),
                     reads=[P.B("scwin", s_c), P.B("h", k, ti)], writes=[P.B("bank", bs)])
            for k in range(KD):
                P.op("pe", MM(pu[:, 0:w], win[s_u][:, k, :], S.h[:, k, c0:c0 + w], k == 0, k == KD - 1),
                     reads=[P.B("scwin", s_u), P.B("h", k, ti)], writes=[P.B("bank", bs + 1)])
            tq = nxt(S, "sctmp", 2)
            P.op("act", ACT(tmp[tq][:, 0:w], pc[:, 0:w], AF.Copy), reads=[P.B("bank", bs)], writes=[P.B("sctmp", tq)])
            P.op("dve", TTo(cu[cq][:, c0:c0 + w], pu[:, 0:w], tmp[tq][:, 0:w], ALU.mult),
                 reads=[P.B("bank", bs + 1), P.B("sctmp", tq)], writes=[P.B("cu", cq, ti)])
        cub = [P.B("cu", cq, ti) for ti in range(len(TILES_H))]
        vq = 0
        P.op("act", ACT(v[vq][:, 0:T], cu[cq][:, HALO:TT], AF.Identity, scale=sm(S, cwcol + m * 3 + 2)),
             reads=cub + [P.B("smalls")], writes=[P.B("v", vq)])
        P.op("dve", STT(v[vq][:, 0:T], cu[cq][:, HALO - 1:TT - 1], sm(S, cwcol + m * 3 + 1), v[vq][:, 0:T], ALU.mult, ALU.add),
             reads=cub + [P.B("v", vq)], writes=[P.B("v", vq)])
        P.op("dve", STT(v[vq][:, 0:T], cu[cq][:, HALO - 2:TT - 2], sm(S, cwcol + m * 3 + 0), v[vq][:, 0:T], ALU.mult, ALU.add),
             reads=cub + [P.B("v", vq)], writes=[P.B("v", vq)])
        s_b = load_w(m)
        for (c0, w) in TILES:
            ti = tidx(c0)
            bk = 4 + nxt(S, "dbank", 4)
            bank = S.banks[bk]
            for k in range(KD):
                P.op("pe", MM(bank[:, 0:w], win[s_b][:, k, :], S.h[:, k, c0:c0 + w], k == 0, k == KD - 1),
                     reads=[P.B("scwin", s_b), P.B("h", k, ti)], writes=[P.B("bank", bk)])
            P.op("dve", TTo(bv[:, m, c0 - HALO:c0 - HALO + w], bank[:, 0:w], v[vq][:, c0 - HALO:c0 - HALO + w], ALU.mult),
                 reads=[P.B("bank", bk), P.B("v", vq)], writes=[P.B("bv", m, ti)])
    for dc in range(KD):
        ws = nxt(S, "scwo", 2)
        P.dma("pool", DMA(wo[ws][:], wout_d[dc]), writes=[P.B("scwo", ws)])
        for (c0, w) in TILES:
            ti = tidx(c0)
            bk = 4 + nxt(S, "dbank", 4)
            bank = S.banks[bk]
            for q in range(KD):
                P.op("pe", MM(bank[:, 0:w], wo[ws][:, q, :], bv[:, q, c0 - HALO:c0 - HALO + w], q == 0, q == KD - 1),
                     reads=[P.B("scwo", ws), P.B("bv", q, ti)], writes=[P.B("bank", bk)])
            P.op("dve", TTo(S.x[:, dc, c0:c0 + w], bank[:, 0:w], S.x[:, dc, c0:c0 + w], ALU.add),
                 reads=[P.B("bank", bk), P.B("x", dc, ti)], writes=[P.B("x", dc, ti)])
    P.phase_end()


import os
SSD_STAGE = int(os.environ.get("SSD_STAGE", "99"))


def emit_ssd(P, S, Wd, C, full, CT=2):
    W = CT * 128
    NT = T // W
    P.phase_begin()
    A = P.psbuf
    win = [A("win%d" % i, [128, KD, 128], BF16) for i in range(3)]
    ht = [A("ht%d" % i, [128, KD, W], BF16) for i in range(2)]
    raw = [A("raw%d" % i, [128, 3 + W], F32) for i in range(3)]
    acc = [A("acc%d" % i, [128, W], F32) for i in range(2)]
    halo = A("halo", [128, 24, 3], F32)
    xTr = [A("xTr%d" % i, [128, W], BF16) for i in range(2)]
    BC = A("BC", [128, 8, W], BF16)
    xs_tok = A("xs_tok", [128, CT, DIN], BF16)
    B_tok = A("B_tok", [128, CT, 512], BF16)
    wdt = A("wdt", [128, KD, 32], BF16)
    e1 = A("e1", [128, W], F32)
    dtT = A("dtT", [128, W], F32)
    adtT = A("adtT", [128, W], F32)
    dtk = A("dtk", [128, 64], F32)
    s32 = A("s32", [128, 8, 32], F32)
    acol = A("acol", [128, 2], F32)
    xdt = A("xdt", [128, DIN], BF16)
    xdtw = A("xdtw", [128, DIN], BF16)
    R = A("R", [128, DIN], F32)
    Rbf = A("Rbf", [128, DIN], BF16)
    Dacc = A("Dacc", [128, 32], F32)
    if full:
        xsD = A("xsD", [128, DIN], BF16)
        wz = [A("wz%d" % i, [128, KD, 512], BF16) for i in range(1)]
        wo = [A("wo%d" % i, [128, 16, 128], BF16) for i in range(2)]
        zs = A("zs", [128, CT, DIN], BF16)
        E = [A("E%d" % i, [128, 8, 128], F32) for i in range(1)]
        M = [A("M%d" % i, [128, 8, 128], BF16) for i in range(2)]
        t1 = [A("t1%d" % i, [128, 512], F32) for i in range(2)]
        junk = A("junk", [128, 512], F32)
        junk2 = A("junk2", [128, 512], BF16)
        ss = A("ss", [128, 8], F32)
        gn_tok = [A("gn_tok%d" % i, [128, 512], BF16) for i in range(2)]
        gnT = A("gnT", [128, 16, W], BF16)
        NEG4 = A("NEG4", [128, 512], BF16)
    bk = S.banks
    TB = bk[2][:, :].bitcast(BF16)
    TBh = [TB[:, 0:512], TB[:, 512:1024]]
    identb = S.cbf[:, 1, :]
    ident32 = S.cf32[0:32, 0, 0:32]
    triT = S.cf32[:, 1, :]
    ones32 = S.cf32[:, 2, :]
    smb = P.B("smalls")

    def bcast(ap32, n):
        return ap32.unsqueeze(2).to_broadcast([128, n, HP])

    P.dma("pool", DMA(wdt[:], Wd["wdt"]), writes=[P.B("wdt")])
    P.op("act", ACT(acol[0:32, 0:1], sm(S, C["alog"], 1, 32), AF.Exp), reads=[smb], writes=[P.B("acol")])
    P.op("dve", TS(acol[0:32, 0:1], acol[0:32, 0:1], -1.0, None, ALU.mult), reads=[P.B("acol")], writes=[P.B("acol")])
    P.op("dve", lambda e: e.memset(Dacc[:], 0.0), writes=[P.B("Dacc")])
    for g in range(NG):
        P.op("pool", lambda e, g=g: e.memset(R[:, g * 512:(g + 1) * 512], 0.0), writes=[P.B("R", g)])
    if full:
        for i in range(4):
            P.op("dve", CP(NEG4[:, i * 128:(i + 1) * 128], S.cf32[:, 3, :]), reads=[P.B("cf32")], writes=[P.B("NEG4")])
        Ef = E[0][:, :, :].rearrange("p a b -> p (a b)")
        for j in range(NCORES - 1):
            P.op("act", ACT(s32[:, 6, :], sm(S, C["dslot"] + 32 * j, 32), AF.Exp), reads=[smb], writes=[P.B("s32", 6)])
            for hf in range(2):
                P.dma("sp", DMA(Ef, Wd["sslot"][j][:, hf * 1024:(hf + 1) * 1024]), writes=[P.B("E", 0)])
                for gg in range(2):
                    g = hf * 2 + gg
                    Rv = R[:, g * 512:(g + 1) * 512].rearrange("p (h d) -> p h d", h=8)
                    P.op("dve", TTo(Rv, Rv, bcast(s32[:, 6, 8 * g:8 * g + 8], 8), ALU.mult),
                         reads=[P.B("R", g), P.B("s32", 6)], writes=[P.B("R", g)])
                    P.op("dve", TTo(R[:, g * 512:(g + 1) * 512], R[:, g * 512:(g + 1) * 512], Ef[:, gg * 512:(gg + 1) * 512], ALU.add),
                         reads=[P.B("R", g), P.B("E", 0)], writes=[P.B("R", g)])
    for g in range(NG):
        P.op("act", ACT(Rbf[:, g * 512:(g + 1) * 512], R[:, g * 512:(g + 1) * 512], AF.Copy), reads=[P.B("R", g)], writes=[P.B("Rbf", g)])

    mlist = list(range(24)) if full else list(range(20))

    def load_win(m):
        s = nxt(S, "ssdwin", 3)
        P.dma("pool", DMA(win[s][:], Wd["wx"][m]), writes=[P.B("ssdwin", s)])
        return s

    hq = nxt(S, "ht", 2)
    emit_norm(P, S, C["nw"], [(0, HALO)], dst=lambda k, c0, w: (ht[hq][:, k, 0:w], [P.B("ht", hq, k)]), banks=(3,))
    for m in mlist:
        s = load_win(m)
        b = nxt(S, "ipbank", 2)
        for k in range(KD):
            P.op("pe", MM(bk[b][:, 0:HALO], win[s][:, k, :], ht[hq][:, k, 0:HALO], k == 0, k == KD - 1),
                 reads=[P.B("ssdwin", s), P.B("ht", hq, k)], writes=[P.B("bank", b)])
        P.op("act", ACT(halo[:, m, :], bk[b][:, 1:HALO], AF.Copy), reads=[P.B("bank", b)], writes=[P.B("halo", m)])

    for t in range(NT):
        c0 = HALO + t * W
        ti = tidx(HALO + ((t * W) // 512) * 512)
        hq = nxt(S, "ht", 2)
        emit_norm(P, S, C["nw"], [(c0, W)], dst=lambda k, c0_, w, hq=hq: (ht[hq][:, k, 0:w], [P.B("ht", hq, k)]), banks=(3,))
        htb = [P.B("ht", hq, k) for k in range(KD)]
        for k in range(KD):
            P.op("pe", MM(bk[3][0:32, 0:W], wdt[:, k, :], ht[hq][:, k, 0:W], k == 0, k == KD - 1),
                 reads=[P.B("wdt"), P.B("ht", hq, k)], writes=[P.B("bank", 3)])
        P.op("act", ACT(e1[0:32, 0:W], bk[3][0:32, 0:W], AF.Exp, bias=sm(S, C["dtb"], 1, 32)),
             reads=[P.B("bank", 3), smb], writes=[P.B("e1")])
        P.op("act", ACT(dtT[0:32, 0:W], e1[0:32, 0:W], AF.Ln, bias=sm(S, C["one"], 1, 32)),
             reads=[P.B("e1"), smb], writes=[P.B("dtT")])
        P.op("dve", TS(adtT[0:32, 0:W], dtT[0:32, 0:W], acol[0:32, 0:1], None, ALU.mult),
             reads=[P.B("dtT"), P.B("acol")], writes=[P.B("adtT")])
        if full:
            for zq in range(4):
                P.dma("pool", DMA(wz[0][:], Wd["wz"][zq]), writes=[P.B("wz", 0)])
                for c in range(CT):
                    b = nxt(S, "ipbank", 2)
                    for k in range(KD):
                        P.op("pe", MM(bk[b][:, 0:512], ht[hq][:, k, c * 128:(c + 1) * 128], wz[0][:, k, :], k == 0, k == KD - 1),
                             reads=[P.B("wz", 0), P.B("ht", hq, k)], writes=[P.B("bank", b)])
                    P.op("act", ACT(zs[:, c, zq * 512:(zq + 1) * 512], bk[b][:, 0:512], AF.Silu),
                         reads=[P.B("bank", b)], writes=[P.B("zs", c, zq)])
        for m in (mlist if SSD_STAGE >= 3 else []):
            s = load_win(m)
            b = nxt(S, "ipbank", 2)
            for k in range(KD):
                P.op("pe", MM(bk[b][:, 0:W], win[s][:, k, :], ht[hq][:, k, 0:W], k == 0, k == KD - 1),
                     reads=[P.B("ssdwin", s), P.B("ht", hq, k)], writes=[P.B("bank", b)])
            r = nxt(S, "raw", 3)
            P.op("act", ACT(raw[r][:, 3:3 + W], bk[b][:, 0:W], AF.Copy), reads=[P.B("bank", b)], writes=[P.B("raw", r)])
            P.op("pool", CP(raw[r][:, 0:3], halo[:, m, :]), reads=[P.B("halo", m)], writes=[P.B("rawh", r)])
            P.op("pool", CP(halo[:, m, :], raw[r][:, W:W + 3]), reads=[P.B("raw", r)], writes=[P.B("halo", m)])
            aq = nxt(S, "acc", 2)
            cw = C["cw"] + m * 5
            P.op("act", ACT(acc[aq][:, 0:W], raw[r][:, 3:3 + W], AF.Identity, scale=sm(S, cw + 3), bias=sm(S, cw + 4)),
                 reads=[P.B("raw", r), smb], writes=[P.B("acc", aq)])
            for kk in (2, 1, 0):
                P.op("dve", STT(acc[aq][:, 0:W], raw[r][:, kk:kk + W], sm(S, cw + kk), acc[aq][:, 0:W], ALU.mult, ALU.add),
                     reads=[P.B("raw", r), P.B("rawh", r), P.B("acc", aq), smb], writes=[P.B("acc", aq)])
            if m < 16:
                xq = nxt(S, "xTr", 2)
                P.op("act", ACT(xTr[xq][:, 0:W], acc[aq][:, 0:W], AF.Silu), reads=[P.B("acc", aq)], writes=[P.B("xTr", xq)])
                if SSD_STAGE < 4:
                    continue
                tq = nxt(S, "tbh", 2)
                for c in range(CT):
                    P.op("pe", TR(TBh[tq][:, c * 128:(c + 1) * 128], xTr[xq][:, c * 128:(c + 1) * 128], identb),
                         reads=[P.B("xTr", xq), P.B("cbf")], writes=[P.B("tbh", tq)])
                P.op("dve", CP(xs_tok[:, 0:CT, m * 128:(m + 1) * 128],
                               TBh[tq][:, 0:CT * 128].rearrange("p (c j) -> p c j", c=CT)),
                     reads=[P.B("tbh", tq)], writes=[P.B("xs_tok", c_, m // 4) for c_ in range(CT)])
            else:
                P.op("act", ACT(BC[:, m - 16, 0:W], acc[aq][:, 0:W], AF.Silu), reads=[P.B("acc", aq)], writes=[P.B("BC", m - 16)])
                if m < 20 and SSD_STAGE >= 4:
                    tq = nxt(S, "tbh", 2)
                    for c in range(CT):
                        P.op("pe", TR(TBh[tq][:, c * 128:(c + 1) * 128], BC[:, m - 16, c * 128:(c + 1) * 128], identb),
                             reads=[P.B("BC", m - 16), P.B("cbf")], writes=[P.B("tbh", tq)])
                    P.op("dve", CP(B_tok[:, 0:CT, (m - 16) * 128:(m - 15) * 128],
                                   TBh[tq][:, 0:CT * 128].rearrange("p (c j) -> p c j", c=CT)),
                         reads=[P.B("tbh", tq)], writes=[P.B("B_tok", c_, m - 16) for c_ in range(CT)])
        for c in (range(CT) if SSD_STAGE >= 5 else []):
            lc = c * 128
            P.op("pe", MM(bk[3][:, 0:32], dtT[0:32, lc:lc + 128], ident32), reads=[P.B("dtT"), P.B("cf32")], writes=[P.B("bank", 3)])
            P.op("pe", MM(bk[3][:, 32:64], adtT[0:32, lc:lc + 128], ident32), reads=[P.B("adtT"), P.B("cf32")], writes=[P.B("bank", 3)])
            P.op("act", ACT(dtk[:, 0:64], bk[3][:, 0:64], AF.Copy), reads=[P.B("bank", 3)], writes=[P.B("dtk")])
            P.op("pe", MM(bk[3][:, 64:96], triT, dtk[:, 32:64]), reads=[P.B("dtk"), P.B("cf32")], writes=[P.B("bank", 3)])
            P.op("pe", MM(bk[3][:, 96:128], ones32, dtk[:, 32:64]), reads=[P.B("dtk"), P.B("cf32")], writes=[P.B("bank", 3)])
            P.op("act", ACT(s32[:, 0, :], bk[3][:, 64:96], AF.Copy), reads=[P.B("bank", 3)], writes=[P.B("s32", 0)])
            P.op("act", ACT(s32[:, 1, :], bk[3][:, 64:96], AF.Exp), reads=[P.B("bank", 3)], writes=[P.B("s32", 1)])
            P.op("act", ACT(s32[:, 2, :], bk[3][:, 96:128], AF.Exp), reads=[P.B("bank", 3)], writes=[P.B("s32", 2)])
            P.op("dve", TTo(s32[:, 5, :], bk[3][:, 96:128], s32[:, 0, :], ALU.subtract),
                 reads=[P.B("bank", 3), P.B("s32", 0)], writes=[P.B("s32", 5)])
            P.op("act", ACT(s32[:, 3, :], s32[:, 5, :], AF.Exp), reads=[P.B("s32", 5)], writes=[P.B("s32", 3)])
            P.op("dve", TS(s32[:, 4, :], s32[:, 0, :], -1.0, None, ALU.mult), reads=[P.B("s32", 0)], writes=[P.B("s32", 4)])
            P.op("dve", TTo(Dacc[:], bk[3][:, 96:128], Dacc[:], ALU.add), reads=[P.B("bank", 3), P.B("Dacc")], writes=[P.B("Dacc")])
            if SSD_STAGE < 6:
                continue
            xsb = [P.B("xs_tok", c, q) for q in range(4)]
            xv = xs_tok[:, c, :].rearrange("p (h d) -> p h d", h=NH)
            P.op("pool", TTo(xdt[:, :].rearrange("p (h d) -> p h d", h=NH), xv, bcast(dtk[:, 0:32], NH), ALU.mult),
                 reads=xsb + [P.B("dtk")], writes=[P.B("xdt")])
            P.op("pool", TTo(xdtw[:, :].rearrange("p (h d) -> p h d", h=NH), xdt[:, :].rearrange("p (h d) -> p h d", h=NH),
                             bcast(s32[:, 3, :], NH), ALU.mult),
                 reads=[P.B("xdt"), P.B("s32", 3)], writes=[P.B("xdtw")])
            if full:
                P.op("dve", TTo(xsD[:, :].rearrange("p (h d) -> p h d", h=NH), xv, bcast(sm(S, C["dskip"], 32), NH), ALU.mult),
                     reads=xsb + [smb], writes=[P.B("xsD")])
                for g in range(NG):
                    P.op("pe", MM(bk[4][:, g * 128:(g + 1) * 128], BC[:, g, lc:lc + 128], BC[:, 4 + g, lc:lc + 128]),
                         reads=[P.B("BC", g), P.B("BC", 4 + g)], writes=[P.B("bank", 4)])
                for g in range(NG):
                    eq = 0
                    for hf in range(2):
                        be = 5 + hf
                        P.op("pe", MM(bk[be][:, 0:512], identb, NEG4[:, :], True, False),
                             reads=[P.B("NEG4"), P.B("cbf")], writes=[P.B("bank", be)])
                        for i in range(4):
                            h = 8 * g + 4 * hf + i
                            P.op("pe", MM(bk[be][:, i * 128:(i + 1) * 128], dtk[:, 32 + h:33 + h].to_broadcast([128, 128]), triT,
                                          False, i == 3),
                                 reads=[P.B("dtk"), P.B("cf32")], writes=[P.B("bank", be)])
                        for i in range(4):
                            h = 8 * g + 4 * hf + i
                            P.op("act", ACT(E[eq][:, 4 * hf + i, :], bk[be][:, i * 128:(i + 1) * 128], AF.Exp, bias=s32[:, 4, h:h + 1]),
                                 reads=[P.B("bank", be), P.B("s32", 4)], writes=[P.B("E", eq)])
                    mq = nxt(S, "M", 2)
                    P.op("dve", TTo(M[mq][:, :, :], bk[4][:, g * 128:(g + 1) * 128].unsqueeze(1).to_broadcast([128, 8, 128]),
                                    E[eq][:, :, :], ALU.mult),
                         reads=[P.B("bank", 4), P.B("E", eq)], writes=[P.B("M", mq)])
                    P.op("pe", MM(bk[7][:, 0:512], identb, xsD[:, g * 512:(g + 1) * 512], True, False),
                         reads=[P.B("xsD"), P.B("cbf")], writes=[P.B("bank", 7)])
                    for i in range(8):
                        h = 8 * g + i
                        P.op("pe", MM(bk[7][:, i * 64:(i + 1) * 64], M[mq][:, i, :], xdt[:, h * 64:(h + 1) * 64], False, i == 7),
                             reads=[P.B("M", mq), P.B("xdt")], writes=[P.B("bank", 7)])
                    b = nxt(S, "ipbank", 2)
                    P.op("pe", MM(bk[b][:, 0:512], BC[:, 4 + g, lc:lc + 128], Rbf[:, g * 512:(g + 1) * 512]),
                         reads=[P.B("BC", 4 + g), P.B("Rbf", g)], writes=[P.B("bank", b)])
                    tq1 = nxt(S, "t1", 2)
                    tt = t1[tq1]
                    P.op("dve", TTo(tt[:, :].rearrange("p (h d) -> p h d", h=8), bk[b][:, 0:512].rearrange("p (h d) -> p h d", h=8),
                                    bcast(s32[:, 1, 8 * g:8 * g + 8], 8), ALU.mult),
                         reads=[P.B("bank", b), P.B("s32", 1)], writes=[P.B("t1", tq1)])
                    P.op("dve", TTo(tt[:, :], bk[7][:, 0:512], tt[:, :], ALU.add),
                         reads=[P.B("bank", 7), P.B("t1", tq1)], writes=[P.B("t1", tq1)])
                    P.op("dve", TTo(tt[:, :], tt[:, :], zs[:, c, g * 512:(g + 1) * 512], ALU.mult),
                         reads=[P.B("t1", tq1), P.B("zs", c, g)], writes=[P.B("t1", tq1)])
                    P.op("act", ACT(junk[:, :], tt[:, :], AF.Square), reads=[P.B("t1", tq1)], writes=[P.B("junk")])
                    P.op("dve", lambda e, g=g: e.tensor_scalar(out=junk2[:, :], in0=junk[:, :], scalar1=1.0, scalar2=None,
                                                               op0=ALU.mult, op1=ALU.add, accum_out=ss[:, g:g + 1]),
                         reads=[P.B("junk")], writes=[P.B("ss", g), P.B("junk2")])
                    P.op("act", ACT(ss[:, 4 + g:5 + g], ss[:, g:g + 1], AF.Sqrt, bias=sm(S, 0), scale=1.0 / 512),
                         reads=[P.B("ss", g), smb], writes=[P.B("ss", 4 + g)])
                    P.op("dve", RCP(ss[:, 4 + g:5 + g], ss[:, 4 + g:5 + g]), reads=[P.B("ss", 4 + g)], writes=[P.B("ss", 4 + g)])
                    gq = nxt(S, "gn_tok", 2)
                    P.op("act", ACT(gn_tok[gq][:, :], tt[:, :], AF.Identity, scale=ss[:, 4 + g:5 + g]),
                         reads=[P.B("t1", tq1), P.B("ss", 4 + g)], writes=[P.B("gn_tok", gq)])
                    tq = nxt(S, "tbh", 2)
                    for j in range(4):
                        P.op("pe", TR(TBh[tq][:, j * 128:(j + 1) * 128], gn_tok[gq][:, j * 128:(j + 1) * 128], identb),
                             reads=[P.B("gn_tok", gq), P.B("cbf")], writes=[P.B("tbh", tq)])
                    for j in range(4):
                        P.op("act", ACT(gnT[:, 4 * g + j, lc:lc + 128], TBh[tq][:, j * 128:(j + 1) * 128], AF.Identity,
                                        scale=sm(S, C["gnw"] + 4 * g + j)),
                             reads=[P.B("tbh", tq), smb], writes=[P.B("gnT", 4 * g + j)])
            for g in (range(NG) if SSD_STAGE >= 7 else []):
                b = nxt(S, "ipbank", 2)
                P.op("pe", MM(bk[b][:, 0:512], B_tok[:, c, g * 128:(g + 1) * 128], xdtw[:, g * 512:(g + 1) * 512]),
                     reads=[P.B("B_tok", c, g), P.B("xdtw")], writes=[P.B("bank", b)])
                Rv = R[:, g * 512:(g + 1) * 512].rearrange("p (h d) -> p h d", h=8)
                P.op("dve", TTo(Rv, Rv, bcast(s32[:, 2, 8 * g:8 * g + 8], 8), ALU.mult),
                     reads=[P.B("R", g), P.B("s32", 2)], writes=[P.B("R", g)])
                P.op("dve", TTo(R[:, g * 512:(g + 1) * 512], bk[b][:, 0:512], R[:, g * 512:(g + 1) * 512], ALU.add),
                     reads=[P.B("bank", b), P.B("R", g)], writes=[P.B("R", g)])
                P.op("act", ACT(Rbf[:, g * 512:(g + 1) * 512], R[:, g * 512:(g + 1) * 512], AF.Copy),
                     reads=[P.B("R", g)], writes=[P.B("Rbf", g)])
        if full:
            for dc in range(KD):
                ws = nxt(S, "ssdwo", 2)
                P.dma("pool", DMA(wo[ws][:], Wd["wo"][dc]), writes=[P.B("ssdwo", ws)])
                b = nxt(S, "ipbank", 2)
                for q in range(16):
                    P.op("pe", MM(bk[b][:, 0:W], wo[ws][:, q, :], gnT[:, q, 0:W], q == 0, q == 15),
                         reads=[P.B("ssdwo", ws), P.B("gnT", q)], writes=[P.B("bank", b)])
                P.op("dve", TTo(S.x[:, dc, c0:c0 + W], bk[b][:, 0:W], S.x[:, dc, c0:c0 + W], ALU.add),
                     reads=[P.B("bank", b), P.B("x", dc, ti)], writes=[P.B("x", dc, ti)])
    outs = []
    if not full:
        P.dma("sp", DMA(Wd["s_out"], R[:, :]), reads=[P.B("R", g) for g in range(NG)], writes=[P.B("s_out")], sem="s_out")
        P.dma("sp", DMA(Wd["d_out"], Dacc[:, :]), reads=[P.B("Dacc")], writes=[P.B("d_out")], sem="d_out")
        outs = [P.B("s_out"), P.B("d_out")]
    P.phase_end()
    return outs

def tile_w_in(W, ncols_pad=None):
    C = W.shape[1]
    return np.ascontiguousarray(W.reshape(KD, 128, C // 128, 128).transpose(2, 1, 0, 3))


def tile_rows(W):
    return np.ascontiguousarray(W.reshape(W.shape[0] // 128, 128, W.shape[1]))


def vec_cols(v):
    return np.ascontiguousarray(v.reshape(-1, 128).T)


def make_consts():
    c = np.zeros((128, 4, 128), np.float32)
    c[:, 0, :] = np.eye(128, dtype=np.float32)
    c[:, 1, :] = np.triu(np.ones((128, 128), np.float32))
    c[:, 2, :] = 1.0
    c[:, 3, :] = np.tril(np.full((128, 128), -30000.0, np.float32), -1)
    return c


def shard_x_fm(xfull, halo=True):
    outs = []
    for c in range(NCORES):
        blk = np.zeros((TT, D), np.float32)
        lo = c * T
        blk[HALO:] = xfull[lo:lo + T]
        if c > 0:
            blk[:HALO] = xfull[lo - HALO:lo]
        outs.append(np.ascontiguousarray(blk.T.reshape(KD, 128, TT).transpose(1, 0, 2)))
    return outs


def unshard_x_fm(shards):
    rows = []
    for s in shards:
        rows.append(s.transpose(1, 0, 2).reshape(D, -1).T)
    return np.ascontiguousarray(np.concatenate(rows, axis=0))


class Cols:
    def __init__(self):
        self.n = 0
        self.d = {}

    def add(self, name, n):
        self.d[name] = self.n
        self.n += n
        return self.d[name]

    def __getitem__(self, k):
        return self.d[k]


def cols_A():
    c = Cols()
    for name, n in [("eps", 1), ("one", 1), ("nwf", 8), ("nw", 8), ("alog", 1), ("dtb", 1), ("cw", 120)]:
        c.add(name, n)
    return c


def cols_B():
    c = Cols()
    for name, n in [("eps", 1), ("one", 1), ("nw", 8), ("nwf", 8), ("alog", 1), ("dtb", 1), ("cw", 120),
                    ("dskip", 32), ("gnw", 16), ("dslot", 224)]:
        c.add(name, n)
    return c


def cols_C():
    c = Cols()
    for name, n in [("eps", 1), ("nw1", 8), ("nwm", 8), ("nw2", 8), ("cw", 24), ("fnw", 8)]:
        c.add(name, n)
    return c


def _ffn_inputs(nc, sfx):
    wg = nc.dram_tensor("wg" + sfx, [NF, 128, KD, 128], F32, kind="ExternalInput").ap()
    wu = nc.dram_tensor("wu" + sfx, [NF, 128, KD, 128], F32, kind="ExternalInput").ap()
    wd = nc.dram_tensor("wd" + sfx, [NF, 128, D], F32, kind="ExternalInput").ap()
    return wg, wu, wd


def build_A():
    nc = bass.Bass("TRN2", target_bir_lowering=False)
    C = cols_A()
    x_d = nc.dram_tensor("x", [128, KD, TT], F32, kind="ExternalInput").ap()
    wg, wu, wd = _ffn_inputs(nc, "1")
    Wd = {
        "wx": nc.dram_tensor("wx", [24, 128, KD, 128], F32, kind="ExternalInput").ap(),
        "wdt": nc.dram_tensor("wdt", [128, KD, 32], F32, kind="ExternalInput").ap(),
        "s_out": nc.dram_tensor("s_out", [128, DIN], F32, kind="ExternalOutput").ap(),
        "d_out": nc.dram_tensor("d_out", [128, 32], F32, kind="ExternalOutput").ap(),
    }
    sm_d = nc.dram_tensor("smalls", [128, C.n], F32, kind="ExternalInput").ap()
    c_d = nc.dram_tensor("consts", [128, 4, 128], F32, kind="ExternalInput").ap()
    xo_d = nc.dram_tensor("xo", [128, KD, TT], F32, kind="ExternalOutput").ap()
    P = Prog(nc)
    S = setup_state(P, sm_d, c_d, C.n)
    emit_load_x(P, S, x_d)
    emit_ffn(P, S, wg, wu, wd, C["nwf"], TILES_H)
    outs = emit_store_x(P, S, xo_d, True)
    outs += emit_ssd(P, S, Wd, C, full=False)
    P.final_wait("sp", outs)
    P.emit()
    P.close()
    return nc


def build_B():
    nc = bass.Bass("TRN2", target_bir_lowering=False)
    C = cols_B()
    x_d = nc.dram_tensor("x", [128, KD, TT], F32, kind="ExternalInput").ap()
    wg, wu, wd = _ffn_inputs(nc, "2")
    ss = nc.dram_tensor("sslot", [NCORES - 1, 128, DIN], F32, kind="ExternalInput").ap()
    Wd = {
        "wx": nc.dram_tensor("wx", [24, 128, KD, 128], F32, kind="ExternalInput").ap(),
        "wdt": nc.dram_tensor("wdt", [128, KD, 32], F32, kind="ExternalInput").ap(),
        "wz": nc.dram_tensor("wz", [4, 128, KD, 512], F32, kind="ExternalInput").ap(),
        "wo": nc.dram_tensor("wo", [KD, 128, 16, 128], F32, kind="ExternalInput").ap(),
        "sslot": [ss[j] for j in range(NCORES - 1)],
    }
    sm_d = nc.dram_tensor("smalls", [128, C.n], F32, kind="ExternalInput").ap()
    c_d = nc.dram_tensor("consts", [128, 4, 128], F32, kind="ExternalInput").ap()
    xo_d = nc.dram_tensor("xo", [128, KD, T], F32, kind="ExternalOutput").ap()
    P = Prog(nc)
    S = setup_state(P, sm_d, c_d, C.n)
    emit_load_x(P, S, x_d)
    emit_ssd(P, S, Wd, C, full=True)
    emit_ffn(P, S, wg, wu, wd, C["nwf"], TILES)
    outs = emit_store_x(P, S, xo_d, False)
    P.final_wait("sp", outs)
    P.emit()
    P.close()
    return nc


def build_C():
    nc = bass.Bass("TRN2", target_bir_lowering=False)
    C = cols_C()
    x_d = nc.dram_tensor("x", [128, KD, TT], F32, kind="ExternalInput").ap()
    wg1, wu1, wd1 = _ffn_inputs(nc, "1")
    wg2, wu2, wd2 = _ffn_inputs(nc, "2")
    win_d = nc.dram_tensor("win", [24, 128, KD, 128], F32, kind="ExternalInput").ap()
    wout_d = nc.dram_tensor("wout", [KD, 128, KD, 128], F32, kind="ExternalInput").ap()
    sm_d = nc.dram_tensor("smalls", [128, C.n], F32, kind="ExternalInput").ap()
    c_d = nc.dram_tensor("consts", [128, 4, 128], F32, kind="ExternalInput").ap()
    xo_d = nc.dram_tensor("xo", [128, KD, T], F32, kind="ExternalOutput").ap()
    xn_d = nc.dram_tensor("xn", [128, KD, T], F32, kind="ExternalOutput").ap()
    P = Prog(nc)
    S = setup_state(P, sm_d, c_d, C.n)
    emit_load_x(P, S, x_d)
    emit_ffn(P, S, wg1, wu1, wd1, C["nw1"], TILES_H)
    emit_sc(P, S, win_d, wout_d, C["nwm"], C["cw"])
    emit_ffn(P, S, wg2, wu2, wd2, C["nw2"], TILES)
    outs = emit_store_x(P, S, xo_d, False)
    P.phase_begin()
    stg = [P.psbuf("stg%d" % i, [128, KD, 512], F32) for i in range(2)]
    for (c0, w) in TILES:
        q = nxt(S, "stg", 2)
        emit_norm(P, S, C["fnw"], [(c0, w)],
                  dst=lambda k, c0_, w_, q=q: (stg[q][:, k, 0:w_], [P.B("stg", q, k)]))
        b = P.B("xnstore", c0)
        P.dma("sp", DMA(xn_d[:, :, c0 - HALO:c0 - HALO + w], stg[q][:, :, 0:w]),
              reads=[P.B("stg", q, k) for k in range(KD)], writes=[b], sem="xnst%d" % q)
        outs.append(b)
    P.phase_end()
    P.final_wait("sp", outs)
    P.emit()
    P.close()
    return nc


_PROGS = {}


def _prog(name):
    if name not in _PROGS:
        _PROGS[name] = {"A": build_A, "B": build_B, "C": build_C}[name]()
    return _PROGS[name]


def _run(name, in_maps):
    res = run_bass_kernel_spmd(_prog(name), in_maps, core_ids=list(range(NCORES)))
    return res.results


def _rows32(v):
    o = np.zeros((128,), np.float32)
    o[:32] = v
    return o


def _rep(v):
    return np.ascontiguousarray(np.broadcast_to(np.asarray(v, np.float32)[None, :], (128, len(v))))


def _ffn_maps(sfx, wg, wu, wd):
    return {"wg" + sfx: tile_w_in(wg), "wu" + sfx: tile_w_in(wu), "wd" + sfx: tile_rows(wd)}


def kernel(x, norm_w, ffn_w_gate, ffn_w_up, ffn_w_down, ssd_w_in, ssd_conv_w, ssd_conv_b,
           ssd_dt_bias, ssd_a_log, ssd_d, ssd_norm_w, ssd_w_out, sc_w_in, sc_conv_w,
           sc_w_out, final_norm_w):
    f = lambda a: np.asarray(a, dtype=np.float32)
    x = f(x)
    norm_w, ffn_w_gate, ffn_w_up, ffn_w_down = f(norm_w), f(ffn_w_gate), f(ffn_w_up), f(ffn_w_down)
    consts = make_consts()
    cur = x[0]
    out = None
    for i in range(4):
        j = i // 2
        if i % 2 == 0:
            w_in = f(ssd_w_in[j])
            wx = tile_w_in(w_in[:, DIN:DIN + CONV])
            wdt = np.ascontiguousarray(w_in[:, DIN + CONV:].reshape(KD, 128, 32).transpose(1, 0, 2))
            wz = np.ascontiguousarray(w_in[:, :DIN].reshape(KD, 128, 4, 512).transpose(2, 1, 0, 3))
            wo = np.ascontiguousarray(f(ssd_w_out[j]).reshape(16, 128, KD, 128).transpose(2, 1, 0, 3))
            cw = f(ssd_conv_w[j])
            cb = f(ssd_conv_b[j])
            cwt = np.zeros((128, 24, 5), np.float32)
            for k in range(4):
                cwt[:, :, k] = vec_cols(cw[k])
            cwt[:, :, 4] = vec_cols(cb)
            CA = cols_A()
            sa = np.zeros((128, CA.n), np.float32)
            sa[:, CA["eps"]] = EPS
            sa[:, CA["one"]] = 1.0
            sa[:, CA["nwf"]:CA["nwf"] + 8] = vec_cols(norm_w[i, 0])
            sa[:, CA["nw"]:CA["nw"] + 8] = vec_cols(norm_w[i, 1])
            sa[:, CA["alog"]] = _rows32(f(ssd_a_log[j]))
            sa[:, CA["dtb"]] = _rows32(f(ssd_dt_bias[j]))
            sa[:, CA["cw"]:CA["cw"] + 120] = cwt.reshape(128, 120)
            xs = shard_x_fm(cur)
            base = {"wx": wx, "wdt": wdt, "smalls": sa, "consts": consts}
            base.update(_ffn_maps("1", ffn_w_gate[i, 0], ffn_w_up[i, 0], ffn_w_down[i, 0]))
            ra = _run("A", [dict(base, x=xs[c]) for c in range(NCORES)])
            CB = cols_B()
            base = {"wx": wx, "wdt": wdt, "wz": wz, "wo": wo, "consts": consts}
            base.update(_ffn_maps("2", ffn_w_gate[i, 1], ffn_w_up[i, 1], ffn_w_down[i, 1]))
            maps = []
            for c in range(NCORES):
                sb = np.zeros((128, CB.n), np.float32)
                sb[:, CB["eps"]] = EPS
                sb[:, CB["one"]] = 1.0
                sb[:, CB["nw"]:CB["nw"] + 8] = vec_cols(norm_w[i, 1])
                sb[:, CB["nwf"]:CB["nwf"] + 8] = vec_cols(norm_w[i, 2])
                sb[:, CB["alog"]] = _rows32(f(ssd_a_log[j]))
                sb[:, CB["dtb"]] = _rows32(f(ssd_dt_bias[j]))
                sb[:, CB["cw"]:CB["cw"] + 120] = cwt.reshape(128, 120)
                sb[:, CB["dskip"]:CB["dskip"] + 32] = _rep(f(ssd_d[j]))
                sb[:, CB["gnw"]:CB["gnw"] + 16] = vec_cols(f(ssd_norm_w[j]))
                sslot = np.zeros((NCORES - 1, 128, DIN), np.float32)
                for q in range(c):
                    slot = NCORES - 1 - c + q
                    sslot[slot] = ra[q]["s_out"]
                    sb[:, CB["dslot"] + 32 * slot:CB["dslot"] + 32 * slot + 32] = ra[q]["d_out"]
                maps.append(dict(base, x=ra[c]["xo"], smalls=sb, sslot=sslot))
            rb = _run("B", maps)
            cur = unshard_x_fm([r["xo"] for r in rb])
        else:
            CC = cols_C()
            sc = np.zeros((128, CC.n), np.float32)
            sc[:, CC["eps"]] = EPS
            sc[:, CC["nw1"]:CC["nw1"] + 8] = vec_cols(norm_w[i, 0])
            sc[:, CC["nwm"]:CC["nwm"] + 8] = vec_cols(norm_w[i, 1])
            sc[:, CC["nw2"]:CC["nw2"] + 8] = vec_cols(norm_w[i, 2])
            cw = f(sc_conv_w[j])
            cwt = np.stack([vec_cols(cw[k]) for k in range(3)], axis=2)
            sc[:, CC["cw"]:CC["cw"] + 24] = cwt.reshape(128, 24)
            sc[:, CC["fnw"]:CC["fnw"] + 8] = vec_cols(f(final_norm_w))
            base = {"win": tile_w_in(f(sc_w_in[j])), "wout": tile_w_in(f(sc_w_out[j])), "smalls": sc, "consts": consts}
            base.update(_ffn_maps("1", ffn_w_gate[i, 0], ffn_w_up[i, 0], ffn_w_down[i, 0]))
            base.update(_ffn_maps("2", ffn_w_gate[i, 1], ffn_w_up[i, 1], ffn_w_down[i, 1]))
            xs = shard_x_fm(cur)
            rc = _run("C", [dict(base, x=xs[c]) for c in range(NCORES)])
            cur = unshard_x_fm([r["xo"] for r in rc])
            out = unshard_x_fm([r["xn"] for r in rc])
    return out[None].astype(np.float32)
```
